# Optimizing a Trainium2 kernel written in Bass

```python
import jax, jax.numpy as jnp
from jax import lax
import numpy as np

D_MODEL = 2048
BATCH = 8
SEQ = 4096
DEPTH = 4

N_HEADS_A = 8
HEAD_DIM_A = 128
WIDTH_A = N_HEADS_A * HEAD_DIM_A
DILATED_PAIRS = ((128, 1), (512, 4), (2048, 16))
ATTN_BLOCK = 128
N_GROUPS_B = 8
GROUP_DIM_B = 128
WIDTH_B = N_GROUPS_B * GROUP_DIM_B
CHUNK_B = 128
AB_IN = 3 * WIDTH_A + 2 * WIDTH_B
AB_OUT = WIDTH_A + WIDTH_B
CONV_WIDTH = 31
D_FF = ((8 * D_MODEL // 3 + 255) // 256) * 256
N_EVEN = (DEPTH + 1) // 2
N_ODD = DEPTH // 2
EPS = 1e-6

kernel_name = "hybrid_dilated_gmlp_conformer_trunk"


def rms_norm(x, g):
    xf = x.astype(jnp.float32)
    y = xf * lax.rsqrt(jnp.mean(xf * xf, axis=-1, keepdims=True) + EPS)
    return (y * g.astype(jnp.float32)).astype(x.dtype)


def layer_norm(x, g, b):
    xf = x.astype(jnp.float32)
    mu = jnp.mean(xf, axis=-1, keepdims=True)
    var = jnp.mean(jnp.square(xf - mu), axis=-1, keepdims=True)
    y = (xf - mu) * lax.rsqrt(var + EPS)
    return (y * g.astype(jnp.float32) + b.astype(jnp.float32)).astype(x.dtype)


def dilated_branch(q, k, v, window, dilation):
    bsz, seq, nh, hd = q.shape
    sub_len = seq // dilation
    n_blk = -(-sub_len // ATTN_BLOCK)
    pad = n_blk * ATTN_BLOCK - sub_len
    sub_win = window // dilation

    def to_sub(t):
        t = t.reshape(bsz, sub_len, dilation, nh, hd).transpose(0, 2, 1, 3, 4)
        t = t.reshape(bsz * dilation, sub_len, nh, hd)
        t = jnp.pad(t, ((0, 0), (0, pad), (0, 0), (0, 0)))
        return t.reshape(bsz * dilation, n_blk, ATTN_BLOCK, nh, hd)

    def with_prev(t):
        prev = jnp.pad(t[:, :-1], ((0, 0), (1, 0), (0, 0), (0, 0), (0, 0)))
        return jnp.concatenate([prev, t], axis=2)

    qb, kb, vb = to_sub(q), to_sub(k), to_sub(v)
    kk, vv = with_prev(kb), with_prev(vb)
    s = jnp.einsum('bnihd,bnjhd->bnhij', qb, kk).astype(jnp.float32) * (hd ** -0.5)
    i = jnp.arange(ATTN_BLOCK)[:, None]
    j = jnp.arange(2 * ATTN_BLOCK)[None, :]
    dist = i + ATTN_BLOCK - j
    band = (dist >= 0) & (dist <= sub_win)
    key_exists = (jnp.arange(n_blk)[:, None, None] > 0) | (j[None] >= ATTN_BLOCK)
    mask = band[None] & key_exists
    s = jnp.where(mask[None, :, None], s, -jnp.inf)
    m = jnp.max(s, axis=-1, keepdims=True)
    p = jnp.exp(s - m)
    den = jnp.sum(p, axis=-1, keepdims=True)
    o = jnp.einsum('bnhij,bnjhd->bnihd', p, vv.astype(jnp.float32))
    o = o / den.transpose(0, 1, 3, 2, 4)
    lse = (m + jnp.log(den))[..., 0].transpose(0, 1, 3, 2)

    def from_sub(t):
        t = t.reshape((bsz * dilation, n_blk * ATTN_BLOCK) + t.shape[3:])[:, :sub_len]
        t = t.reshape((bsz, dilation, sub_len) + t.shape[2:])
        t = jnp.swapaxes(t, 1, 2)
        return t.reshape((bsz, seq) + t.shape[3:])

    return from_sub(o), from_sub(lse)


def dilated_attention(q, k, v):
    outs, lses = [], []
    for window, dilation in DILATED_PAIRS:
        o, lse = dilated_branch(q, k, v, window, dilation)
        outs.append(o)
        lses.append(lse)
    w = jax.nn.softmax(jnp.stack(lses, axis=0), axis=0)
    out = w[0][..., None] * outs[0]
    for n in range(1, len(outs)):
        out = out + w[n][..., None] * outs[n]
    return out


def mixer_ab(h, w_in, w_s, b_s, ln_g, ln_b, w_out):
    bsz, seq, _ = h.shape
    z = h @ w_in
    q, k, v, u, g = jnp.split(z, [WIDTH_A, 2 * WIDTH_A, 3 * WIDTH_A, 3 * WIDTH_A + WIDTH_B], axis=-1)
    hs = (bsz, seq, N_HEADS_A, HEAD_DIM_A)
    a = dilated_attention(q.reshape(hs), k.reshape(hs), v.reshape(hs))
    a = a.reshape(bsz, seq, WIDTH_A).astype(h.dtype)
    u = jax.nn.gelu(u)
    g = layer_norm(jax.nn.gelu(g), ln_g, ln_b)
    g = g.reshape(bsz, seq // CHUNK_B, CHUNK_B, N_GROUPS_B, GROUP_DIM_B)
    tril = jnp.tril(jnp.ones((CHUNK_B, CHUNK_B), dtype=bool))
    w_causal = jnp.where(tril[None], w_s, jnp.zeros_like(w_s))
    sp = jnp.einsum('hij,bcjhe->bcihe', w_causal, g) + b_s[:, :, None]
    bm = u * sp.reshape(bsz, seq, WIDTH_B)
    return jnp.concatenate([a, bm], axis=-1) @ w_out


def conformer_conv(h, w_pw1, b_pw1, w_dw, b_dw, ln_g, ln_b, w_pw2, b_pw2):
    y = jax.nn.glu(h @ w_pw1 + b_pw1, axis=-1)
    y = lax.conv_general_dilated(
        y, w_dw[:, None, :], window_strides=(1,), padding=((CONV_WIDTH - 1, 0),),
        dimension_numbers=('NWC', 'WIO', 'NWC'), feature_group_count=D_MODEL) + b_dw
    y = jax.nn.silu(layer_norm(y, ln_g, ln_b))
    return y @ w_pw2 + b_pw2


def swiglu(h, w_gate_up, w_down):
    gt, up = jnp.split(h @ w_gate_up, 2, axis=-1)
    return (jax.nn.silu(gt) * up) @ w_down


def setup_inputs(seed: int = 0) -> dict:
    key = jax.random.key(seed)
    ks = jax.random.split(key, 21)

    def nrm(k, shape, scale):
        return jax.random.normal(k, shape, jnp.float32) * scale

    D = D_MODEL
    return {
        "x": nrm(ks[0], (BATCH, SEQ, D), 1.0),
        "c": nrm(ks[1], (BATCH, D), 1.0),
        "w_mod": nrm(ks[2], (DEPTH, D, 6 * D), 0.5 * D ** -0.5),
        "b_mod": nrm(ks[3], (DEPTH, 6 * D), 0.02),
        "norm_g": 1.0 + nrm(ks[4], (DEPTH, 4, D), 0.05),
        "w_in_ab": nrm(ks[5], (N_EVEN, D, AB_IN), D ** -0.5),
        "w_s": nrm(ks[6], (N_EVEN, N_GROUPS_B, CHUNK_B, CHUNK_B), 0.5 * CHUNK_B ** -0.5),
        "b_s": 1.0 + nrm(ks[7], (N_EVEN, CHUNK_B, N_GROUPS_B), 0.1),
        "ln_v_g": 1.0 + nrm(ks[8], (N_EVEN, WIDTH_B), 0.05),
        "ln_v_b": nrm(ks[9], (N_EVEN, WIDTH_B), 0.02),
        "w_out_ab": nrm(ks[10], (N_EVEN, AB_OUT, D), AB_OUT ** -0.5),
        "w_pw1": nrm(ks[11], (N_ODD, D, 2 * D), D ** -0.5),
        "b_pw1": nrm(ks[12], (N_ODD, 2 * D), 0.02),
        "w_dw": nrm(ks[13], (N_ODD, CONV_WIDTH, D), CONV_WIDTH ** -0.5),
        "b_dw": nrm(ks[14], (N_ODD, D), 0.02),
        "ln_c_g": 1.0 + nrm(ks[15], (N_ODD, D), 0.05),
        "ln_c_b": nrm(ks[16], (N_ODD, D), 0.02),
        "w_pw2": nrm(ks[17], (N_ODD, D, D), D ** -0.5),
        "b_pw2": nrm(ks[18], (N_ODD, D), 0.02),
        "w_gate_up": nrm(ks[19], (DEPTH, D, 2 * D_FF), D ** -0.5),
        "w_down": nrm(ks[20], (DEPTH, D_FF, D), D_FF ** -0.5),
    }


def reference(x, c, w_mod, b_mod, norm_g, w_in_ab, w_s, b_s, ln_v_g, ln_v_b, w_out_ab,
              w_pw1, b_pw1, w_dw, b_dw, ln_c_g, ln_c_b, w_pw2, b_pw2, w_gate_up, w_down):
    c_act = jax.nn.silu(c)
    for layer in range(DEPTH):
        mod = (c_act @ w_mod[layer] + b_mod[layer])[:, None, :]
        sh1, sc1, gt1, sh2, sc2, gt2 = jnp.split(mod, 6, axis=-1)
        h = rms_norm(x, norm_g[layer, 0]) * (1.0 + sc1) + sh1
        if layer % 2 == 0:
            e = layer // 2
            y = mixer_ab(h, w_in_ab[e], w_s[e], b_s[e], ln_v_g[e], ln_v_b[e], w_out_ab[e])
        else:
            o = layer // 2
            y = conformer_conv(h, w_pw1[o], b_pw1[o], w_dw[o], b_dw[o], ln_c_g[o], ln_c_b[o],
                               w_pw2[o], b_pw2[o])
        x = x + gt1 * rms_norm(y, norm_g[layer, 1])
        h = rms_norm(x, norm_g[layer, 2]) * (1.0 + sc2) + sh2
        y = swiglu(h, w_gate_up[layer], w_down[layer])
        x = x + gt2 * rms_norm(y, norm_g[layer, 3])
    return x
```

```python
import contextlib
import numpy as np
import ml_dtypes
import concourse.bass as bass
import concourse.mybir as mybir
from concourse.bass_utils import run_bass_kernel_spmd

F32 = mybir.dt.float32
BF16 = mybir.dt.bfloat16
AF = mybir.ActivationFunctionType
ALU = mybir.AluOpType

D = 2048
DC = 16
S = 4096
TT = 512
NT = S // TT
DFF = 5632
FC = 44
NH = 8
EPS = 1e-6
NCORES = 8
ROW = 512
ARENA_BYTES = 211968
DILS = (1, 4, 16)
PA_STOP = 99
OP_STOP = 99

PV_C = 0
PV_BMOD = 16
PV_NG = 400
PV_BPW1 = 656
PV_BDW = 720
PV_LCG = 752
PV_LCB = 784
PV_BPW2 = 816
PV_LVG = 848
PV_LVB = 864
NPV = 880


class T:
    __slots__ = ("w", "rs")

    def __init__(self):
        self.w = None
        self.rs = []


class DmaGroup:
    __slots__ = ("sem", "count", "name")

    def __init__(self, name):
        self.name = name
        self.sem = None
        self.count = 0


class Instr:
    __slots__ = ("eng", "fn", "deps", "group", "target", "ms")

    def __init__(self, eng, fn, group):
        self.eng = eng
        self.fn = fn
        self.deps = []
        self.group = group
        self.target = False
        self.ms = 0


class Sched:
    def __init__(self, nc):
        self.nc = nc
        self.streams = {"pe": [], "act": [], "dve": [], "pool": [], "sp": []}
        self.groups = []
        self.dry = False
        self.n_waits = 0

    def group(self, name):
        g = DmaGroup(name)
        self.groups.append(g)
        return g

    def _dep(self, ins, dep, kind, seen):
        if dep is None or dep is ins or id(dep) in seen:
            return
        seen.add(id(dep))
        if dep.group is None and ins.group is None and dep.eng == ins.eng:
            if dep.eng == "pe" or kind != "raw":
                return
        if dep.group is not None:
            ins.deps.append((dep, dep.group.count))
        else:
            ins.deps.append((dep, 0))

    def op(self, eng, fn, reads=(), writes=(), group=None):
        if self.dry:
            return None
        ins = Instr(eng, fn, group)
        seen = set()
        for t in reads:
            self._dep(ins, t.w, "raw", seen)
        for t in writes:
            self._dep(ins, t.w, "waw", seen)
            for r in t.rs:
                self._dep(ins, r, "war", seen)
        if group is not None:
            group.count += 16
        for t in reads:
            t.rs.append(ins)
        for t in writes:
            t.w = ins
            t.rs = []
        self.streams[eng].append(ins)
        return ins

    def emit(self, final_groups=()):
        nc = self.nc
        for st in self.streams.values():
            for ins in st:
                for d, _ in ins.deps:
                    if d.group is None:
                        d.target = True
        for st in self.streams.values():
            c = 0
            for ins in st:
                if ins.group is None and ins.target:
                    c += 1
                    ins.ms = c
        with contextlib.ExitStack() as es:
            sems = {}
            for e in self.streams:
                sems[e] = es.enter_context(nc.semaphore("s_" + e))
            for g in self.groups:
                if g.count > 0:
                    g.sem = es.enter_context(nc.semaphore("g_" + g.name))
            block = es.enter_context(nc.Block())
            engobj = {"pe": block.tensor, "act": block.scalar, "dve": block.vector,
                      "pool": block.gpsimd, "sp": block.sync}

            def run_stream(ename, e):
                known = {}
                for ins in self.streams[ename]:
                    need = {}
                    for d, v in ins.deps:
                        if d.group is not None:
                            key = id(d.group)
                            sem = d.group.sem
                            val = v
                        else:
                            key = d.eng
                            sem = sems[d.eng]
                            val = d.ms
                        if known.get(key, 0) >= val:
                            continue
                        if key not in need or need[key][1] < val:
                            need[key] = (sem, val)
                    for key, (sem, val) in need.items():
                        e.wait_ge(sem, val)
                        known[key] = val
                        self.n_waits += 1
                    bi = ins.fn(e)
                    if ins.group is not None:
                        bi.then_inc(ins.group.sem, 16)
                    elif ins.target:
                        bi.then_inc(sems[ename], 1)
                if ename == "sp":
                    for g in final_groups:
                        e.wait_ge(g.sem, g.count)

            for ename, deco in engobj.items():
                deco(lambda e, ename=ename: run_stream(ename, e))


class Buf:
    __slots__ = ("ap", "rows")

    def __init__(self, ap, rows):
        self.ap = ap
        self.rows = rows


class Arena:
    def __init__(self, tensor_ap, nbytes, row=ROW):
        self.t = tensor_ap
        self.row = row
        self.nrows = (nbytes + row - 1) // row
        self.rows = [T() for _ in range(self.nrows)]

    def carve(self, off, dtype, shape):
        es = 2 if dtype == BF16 else 4
        n = 1
        for s in shape:
            n *= s
        nb = n * es
        assert off % 4 == 0 and nb % 4 == 0, (off, nb)
        ap = self.t[:, off // 4:(off + nb) // 4]
        if dtype != F32:
            ap = ap.bitcast(dtype)
        return View(self, ap, off, es, list(shape))


class View:
    def __init__(self, arena, flat_ap, off, es, shape):
        self.arena = arena
        self.off = off
        self.es = es
        self.shape = shape
        strides = []
        s = 1
        for d in reversed(shape):
            strides.append(s)
            s *= d
        self.strides = list(reversed(strides))
        if len(shape) == 1:
            self.ap_full = flat_ap
        elif len(shape) == 2:
            self.ap_full = flat_ap.rearrange("p (a b) -> p a b", b=shape[1])
        elif len(shape) == 3:
            self.ap_full = flat_ap.rearrange("p (a b c) -> p a b c", b=shape[1], c=shape[2])
        else:
            raise ValueError(shape)

    def __getitem__(self, idx):
        if not isinstance(idx, tuple):
            idx = (idx,)
        idx = list(idx) + [slice(None)] * (len(self.shape) - len(idx))
        lo = 0
        hi = 0
        for i, ix in enumerate(idx):
            st = self.strides[i]
            if isinstance(ix, int):
                lo += ix * st
                hi += ix * st
            else:
                a, b, c = ix.indices(self.shape[i])
                cnt = len(range(a, b, c))
                assert cnt > 0
                lo += a * st
                hi += (a + (cnt - 1) * c) * st
        b0 = self.off + lo * self.es
        b1 = self.off + hi * self.es + self.es
        rw = self.arena.row
        rows = self.arena.rows[b0 // rw:(b1 - 1) // rw + 1]
        ap = self.ap_full[(slice(None),) + tuple(idx)]
        return Buf(ap, rows)

    def all(self):
        return self[tuple(slice(None) for _ in self.shape)]

    @property
    def ap(self):
        return self.ap_full

    @property
    def rows(self):
        return self.all().rows


def build_nc(dbg=0):
    nc = bass.Bass("TRN2", target_bir_lowering=False)

    def din(name, shape, dt=F32):
        return nc.dram_tensor(name, list(shape), dt, kind="ExternalInput").ap()

    def dscr(name, shape, dt, out=False):
        if out:
            return nc.dram_tensor(name, list(shape), dt, kind="ExternalOutput").ap()
        return nc.dram_tensor(name, list(shape), dt).ap()

    xT = din("xT", [DC, 128, S])
    pvec_d = din("pvec", [128, NPV])
    wdw_d = din("wdwT", [2, 128, DC * 31])
    cmat_d = din("cmat", [128, 4 * 128])
    w_mod = din("w_mod", [4, D, 6 * D])
    w_in = din("w_in_ab", [2, D, 5120])
    wsT_d = din("wsT", [2, 128, NH * 128])
    bsT_d = din("bsT", [2, NH * 128])
    w_out = din("w_out_ab", [2, D, D])
    w_pw1 = din("w_pw1", [2, D, 2 * D])
    w_pw2 = din("w_pw2", [2, D, D])
    w_gu = din("w_gate_up", [4, D, 2 * DFF])
    w_dn = din("w_down", [4, DFF, D])
    outT = nc.dram_tensor("outT", [DC, 128, S], F32, kind="ExternalOutput").ap()

    dbg_out = dbg > 0
    w_in_b1 = dscr("w_in_b", [D, 5120], BF16)
    w_out_b1 = dscr("w_out_b", [D, D], BF16)
    w_pw1_b1 = dscr("w_pw1_b", [D, 2 * D], BF16)
    w_pw2_b1 = dscr("w_pw2_b", [D, D], BF16)
    w_gu_b2 = [dscr("w_gu_b%d" % i, [D, 2 * DFF], BF16) for i in range(2)]
    diag_s1 = dscr("diag", [DC, 128, 31 * 128], BF16)
    w_in_b = [w_in_b1, w_in_b1]
    w_out_b = [w_out_b1, w_out_b1]
    w_pw1_b = [w_pw1_b1, w_pw1_b1]
    w_pw2_b = [w_pw2_b1, w_pw2_b1]
    w_gu_b = [w_gu_b2[0], w_gu_b2[1], w_gu_b2[0], w_gu_b2[1]]
    diag_s = [diag_s1, diag_s1]
    qT_s = dscr("qT_s", [NH, 128, S], BF16, dbg_out)
    kT_s = dscr("kT_s", [NH, 128, S], BF16, dbg_out)
    V_s = dscr("V_s", [S, 1024], BF16, dbg_out)
    bmT_s = dscr("bmT_s", [NH, 128, S], BF16, dbg_out)
    aT_s = dscr("aT_s", [NH, 128, S], BF16, dbg_out)

    dbg_d = nc.dram_tensor("dbg_t", [128, 8192], F32, kind="ExternalOutput").ap() if dbg >= 10 else None
    es = contextlib.ExitStack()
    with es:
        arena_t = es.enter_context(nc.sbuf_tensor("arena", [128, ARENA_BYTES // 4], F32))
        psum_t = es.enter_context(nc.psum_tensor("psum", [128, 4096], F32))
        A = Arena(arena_t, ARENA_BYTES)
        PS = Arena(psum_t, 16384, row=2048)
        Sd = Sched(nc)

        off = 0

        def alloc(nbytes):
            nonlocal off
            o = off
            off += (nbytes + ROW - 1) // ROW * ROW
            return o

        o_pvec = alloc(NPV * 4)
        o_cb = alloc(1024)
        o_onesf = alloc(512)
        o_tril = alloc(512)
        o_eps = alloc(512)
        o_modp = alloc(4 * 96 * 4)
        o_coef = alloc(4 * 4 * 16 * 4)
        o_cact = alloc(512)
        o_cbs = alloc(NH * 128 * 4)
        o_wct = alloc(NH * 128 * 2)
        o_halo = alloc(2 * 16 * 32 * 2)
        o_sq = alloc(4 * 512 * 4)
        o_st = alloc(3 * 512 * 4)
        o_ring = alloc(4 * 16384)
        o_xs = alloc(32768)
        o_f1 = alloc(32768)
        o_big = alloc(45056)
        assert off <= ARENA_BYTES, off

        pvec = A.carve(o_pvec, F32, [NPV])
        cb = A.carve(o_cb, BF16, [4, 128])
        onesf = A.carve(o_onesf, F32, [128])
        tril = A.carve(o_tril, F32, [128])
        epsb = A.carve(o_eps, F32, [128])
        modp = A.carve(o_modp, F32, [4, 96])
        coef = A.carve(o_coef, F32, [4, 4, 16])
        cact = A.carve(o_cact, BF16, [16])
        cbs = A.carve(o_cbs, F32, [NH, 128])
        wct = A.carve(o_wct, BF16, [NH, 128])
        halo = A.carve(o_halo, BF16, [2, 16, 32])
        sq = A.carve(o_sq, F32, [4, 512])
        st = A.carve(o_st, F32, [3, 512])
        ring = [A.carve(o_ring + i * 16384, BF16, [8192]) for i in range(4)]
        ring_g = [Sd.group("ring%d" % i) for i in range(4)]
        xs = A.carve(o_xs, F32, [16, 512])
        f1 = A.carve(o_f1, F32, [16, 512])
        hT = A.carve(o_f1, BF16, [16, 512])
        actT = A.carve(o_big, BF16, [FC, 512])
        abuf = A.carve(o_big, BF16, [16, 512])
        yg = A.carve(o_big, BF16, [16, 544])
        hT2 = A.carve(o_big + 18432, BF16, [16, 512])
        pa = o_big
        stg = A.carve(pa, BF16, [2, 512]); pa += 2048
        vstg = A.carve(pa, BF16, [4, 1024]); pa += 8192
        uT = A.carve(pa, BF16, [8, 512]); pa += 8192
        gbuf = A.carve(pa, F32, [2, 1024]); pa += 8192
        gln = A.carve(pa, BF16, [2, 1024]); pa += 4096
        bmstg = A.carve(pa, BF16, [8, 512]); pa += 8192
        bnst = A.carve(pa, F32, [2, 16]); pa += 512
        sptmp = A.carve(pa, F32, [2, 128]); pa += 1024
        assert pa <= o_big + 45056
        tmp_wdw = A.carve(o_big, F32, [16, 31])
        tmp_dg = A.carve(o_big + 2048, BF16, [2, 31, 128])
        tmp_cmat = A.carve(o_big + 20480, F32, [4, 128])
        tmp_ws = A.carve(o_big + 24576, F32, [NH, 128])
        tmp_bs = A.carve(o_big + 28672, F32, [NH, 128])
        ab = o_ring
        qh = [A.carve(ab + i * 8192, BF16, [S]) for i in range(2)]; ab += 16384
        kh = [A.carve(ab + i * 8192, BF16, [S]) for i in range(2)]; ab += 16384
        vd = [[A.carve(ab + (i * 3 + j) * 8192, BF16, [32, 128]) for j in range(3)] for i in range(2)]
        ab += 6 * 8192
        accn = A.carve(ab, F32, [S]); ab += 16384
        accd = A.carve(ab, F32, [S]); ab += 16384
        pT = A.carve(ab, BF16, [4, 256]); ab += 2048
        aTo = A.carve(ab, BF16, [S]); ab += 8192
        assert ab <= ARENA_BYTES, ab

        bank = [PS.carve(i * 2048, F32, [512]) for i in range(8)]
        ps_rr = [0]

        def next_ps():
            b = bank[ps_rr[0] % 4]
            ps_rr[0] += 1
            return b

        g_const = Sd.group("const")
        g_xs = Sd.group("xs")
        g_ab = Sd.group("ab")
        g_st = Sd.group("store")
        g_st4 = Sd.group("store4")
        g_scr = Sd.group("scr")
        g_conv = Sd.group("conv")
        g_att = [Sd.group("att%d" % i) for i in range(2)]
        g_ao = Sd.group("ao")
        g_dg = Sd.group("dg")

        t_out = [T() for _ in range(NT)]
        t_q = [[T() for _ in range(NT)] for _ in range(NH)]
        t_k = [[T() for _ in range(NT)] for _ in range(NH)]
        t_v = [T() for _ in range(S // 128)]
        t_bm = [T() for _ in range(NT)]
        t_a = [T() for _ in range(NH)]
        t_diag1 = [T() for _ in range(DC)]
        t_diag = [t_diag1, t_diag1]

        op = Sd.op

        def mm(ps, lhsT, rhs, start, stop):
            op("pe", lambda e: e.matmul(ps.ap, lhsT=lhsT.ap, rhs=rhs.ap, start=start, stop=stop),
               reads=lhsT.rows + rhs.rows, writes=ps.rows)

        def act(out, in_, func, bias=None, scale=None, extra_reads=()):
            kw = {}
            rd = list(in_.rows) + list(extra_reads)
            if bias is not None:
                kw["bias"] = bias.ap
                rd += bias.rows
            if scale is not None:
                if isinstance(scale, Buf):
                    kw["scale"] = scale.ap
                    rd += scale.rows
                else:
                    kw["scale"] = scale
            op("act", lambda e: e.activation(out=out.ap, in_=in_.ap, func=func, **kw),
               reads=rd, writes=out.rows)

        def tt(eng, out, in0, in1, alu):
            op(eng, lambda e: e.tensor_tensor(out=out.ap, in0=in0.ap, in1=in1.ap, op=alu),
               reads=in0.rows + in1.rows, writes=out.rows)

        def ts(eng, out, in0, s1, op0, s2=None, op1=None):
            rd = list(in0.rows)
            a1 = s1
            if isinstance(s1, Buf):
                a1 = s1.ap
                rd += s1.rows
            a2 = s2
            if isinstance(s2, Buf):
                a2 = s2.ap
                rd += s2.rows
            if op1 is None:
                op(eng, lambda e: e.tensor_scalar(out=out.ap, in0=in0.ap, scalar1=a1, scalar2=None, op0=op0),
                   reads=rd, writes=out.rows)
            else:
                op(eng, lambda e: e.tensor_scalar(out=out.ap, in0=in0.ap, scalar1=a1, scalar2=a2, op0=op0, op1=op1),
                   reads=rd, writes=out.rows)

        def stt(out, in0, scalar, in1, op0, op1):
            rd = list(in0.rows) + list(in1.rows)
            sc = scalar
            if isinstance(scalar, Buf):
                sc = scalar.ap
                rd += scalar.rows
            op("dve", lambda e: e.scalar_tensor_tensor(out=out.ap, in0=in0.ap, scalar=sc, in1=in1.ap, op0=op0, op1=op1),
               reads=rd, writes=out.rows)

        def copy(eng, out, in_):
            if eng == "act":
                act(out, in_, AF.Copy)
            else:
                op(eng, lambda e: e.tensor_copy(out=out.ap, in_=in_.ap), reads=in_.rows, writes=out.rows)

        def dma(q, out_ap, in_ap, reads, writes, group):
            return op(q, lambda e: e.dma_start(out=out_ap, in_=in_ap), reads=reads, writes=writes, group=group)

        pv = lambda c: pvec[c:c + 1]

        plan = []
        ring_state = {"next_issue": 0, "next_get": 0}
        LOOKAHEAD = 2

        def issue_block(i):
            spec = plan[i]
            if spec is None:
                return
            slot = i % 4
            pos = 0
            for (src_ap, deps, nkc, ncols) in spec:
                dst = ring[slot][pos:pos + nkc * ncols]
                dst_ap = dst.ap.rearrange("p (k n) -> p k n", n=ncols)
                q_ = "pool" if src_ap.dtype == F32 else "sp"
                dma(q_, dst_ap, src_ap, reads=deps, writes=dst.rows, group=ring_g[slot])
                pos += nkc * ncols

        def get_block(spec):
            i = ring_state["next_get"]
            ring_state["next_get"] += 1
            if Sd.dry:
                plan.append(spec)
            else:
                while ring_state["next_issue"] < min(len(plan), i + 1 + LOOKAHEAD):
                    issue_block(ring_state["next_issue"])
                    ring_state["next_issue"] += 1
            slot = i % 4
            views = []
            pos = 0
            for (_, _, nkc, ncols) in spec:
                v = A.carve(o_ring + slot * 16384 + pos * 2, BF16, [nkc, ncols])
                views.append(v)
                pos += nkc * ncols
            assert pos <= 8192
            return views

        def wsrc(wb, tlist, k0, nkc, c0, ncols):
            src = wb[k0 * 128:(k0 + nkc) * 128, c0:c0 + ncols].rearrange("(k p) n -> p k n", p=128)
            return (src, tlist, nkc, ncols)

        conv_T = {}
        dst_T = {}
        conv_groups = {}

        def convert(name, src, dst, ncols, npieces):
            key = dst.name if hasattr(dst, "name") else id(dst)
            step = ncols // npieces
            if key not in dst_T:
                dst_T[key] = [T() for _ in range(npieces)]
            ts_ = dst_T[key]
            assert len(ts_) == npieces
            if name not in conv_groups:
                conv_groups[name] = Sd.group("cv_" + name)
            for i in range(npieces):
                dma("pool", dst[:, i * step:(i + 1) * step], src[:, i * step:(i + 1) * step],
                    reads=[], writes=[ts_[i]], group=conv_groups[name])
            conv_T[name] = (ts_, step)

        def cdeps(name, c0, ncols):
            ts_, step = conv_T[name]
            return ts_[c0 // step:(c0 + ncols - 1) // step + 1]

        deferred = []

        def flush_deferred():
            for f in deferred:
                f()
            deferred.clear()

        sq_rr = [0]

        def next_sq():
            b = sq[sq_rr[0] % 4]
            sq_rr[0] += 1
            return b

        def finish_rstd(ps_stat):
            act(st[0], ps_stat, AF.Sqrt, bias=epsb[0:1], scale=1.0 / D)
            op("dve", lambda e: e.reciprocal(out=st[0].ap, in_=st[0].ap), reads=st[0].rows, writes=st[0].rows)

        def rms_from_xs():
            ps_stat = bank[4]
            for c in range(DC):
                s_ = next_sq()
                act(s_, xs[c], AF.Square)
                mm(ps_stat, onesf.all(), s_, c == 0, c == DC - 1)
            finish_rstd(ps_stat)

        def make_hT(l, which):
            ai = 0 if which == 1 else 2
            boff = 0 if which == 1 else 48
            for c in range(DC):
                tmp = st[1 + c % 2]
                tt("dve", tmp, xs[c], st[0], ALU.mult)
                act(hT[c], tmp, AF.Identity, bias=modp[l, boff + c:boff + c + 1],
                    scale=coef[l, ai, c:c + 1])

        def evac_y(ps, c, bias, first, last):
            s_ = next_sq()
            if bias is None:
                copy("dve", f1[c], ps)
            else:
                ts("dve", f1[c], ps, bias, ALU.add)
            act(s_, f1[c], AF.Square)
            deferred.append(lambda: mm(bank[4], onesf.all(), s_, first, last))

        def x_update(l, which):
            ci = 1 if which == 1 else 3
            for c in range(DC):
                tt("dve", f1[c], f1[c], st[0], ALU.mult)
                stt(xs[c], f1[c], coef[l, ci, c:c + 1], xs[c], ALU.mult, ALU.add)

        dma("sp", pvec.all().ap, pvec_d, [], pvec.all().rows, g_const)
        dma("sp", tmp_cmat.all().ap, cmat_d.rearrange("p (a b) -> p a b", b=128), [], tmp_cmat.all().rows, g_const)
        op("dve", lambda e: e.tensor_copy(out=cb[0:3].ap, in_=tmp_cmat[0:3].ap), reads=tmp_cmat.all().rows, writes=cb.all().rows)
        op("dve", lambda e: e.memset(cb[3].ap, 1.0), writes=cb.all().rows)
        op("dve", lambda e: e.tensor_copy(out=tril.all().ap, in_=tmp_cmat[3].ap), reads=tmp_cmat.all().rows, writes=tril.all().rows)
        op("dve", lambda e: e.memset(onesf.all().ap, 1.0), writes=onesf.all().rows)
        op("dve", lambda e: e.memset(epsb.all().ap, EPS), writes=epsb.all().rows)
        op("dve", lambda e: e.memset(halo.all().ap, 0.0), writes=halo.all().rows)
        act(cact.all(), pvec[PV_C:PV_C + 16], AF.Silu)
        ident = cb[0]
        maskP = cb[1]
        maskC = cb[2]
        onesb = cb[3]

        def mod_layer(l):
            psm = bank[5]
            for jb in range(24):
                i = ring_state["next_get"]
                ring_state["next_get"] += 1
                slot = i % 4
                if Sd.dry:
                    plan.append(None)
                else:
                    while ring_state["next_issue"] < i:
                        issue_block(ring_state["next_issue"])
                        ring_state["next_issue"] += 1
                    ring_state["next_issue"] = max(ring_state["next_issue"], i + 1)
                    src = w_mod[l][:, jb * 512:(jb + 1) * 512].rearrange("(k p) n -> p k n", p=128)
                    dst = ring[slot].all()
                    dma("pool", dst.ap.rearrange("p (k n) -> p k n", n=512), src, [], dst.rows, ring_g[slot])
                wv = A.carve(o_ring + slot * 16384, BF16, [16, 512])
                for q in range(4):
                    j = jb * 4 + q
                    for kc in range(DC):
                        mm(psm[j:j + 1], wv[kc, q * 128:(q + 1) * 128], cact[kc:kc + 1], kc == 0, kc == DC - 1)
            tt("dve", modp[l], psm[0:96], pvec[PV_BMOD + l * 96:PV_BMOD + (l + 1) * 96], ALU.add)
            ng = lambda k: pvec[PV_NG + (l * 4 + k) * 16:PV_NG + (l * 4 + k) * 16 + 16]
            stt(coef[l, 0], modp[l, 16:32], 1.0, ng(0), ALU.add, ALU.mult)
            tt("dve", coef[l, 1], modp[l, 32:48], ng(1), ALU.mult)
            stt(coef[l, 2], modp[l, 64:80], 1.0, ng(2), ALU.add, ALU.mult)
            tt("dve", coef[l, 3], modp[l, 80:96], ng(3), ALU.mult)

        def build_diag(o):
            dma("sp", tmp_wdw.all().ap, wdw_d[o].rearrange("p (c w) -> p c w", w=31), [], tmp_wdw.all().rows, g_const)
            for c in range(DC):
                dg = tmp_dg[c % 2]
                for w in range(31):
                    ts("dve", tmp_dg[c % 2, w], ident, tmp_wdw[c, w:w + 1], ALU.mult)
                dma("sp", diag_s[o][c], dg.ap.rearrange("p w n -> p (w n)"), dg.rows, [t_diag[o][c]], g_dg)

        def build_gmlp_consts(e):
            dma("sp", tmp_ws.all().ap, wsT_d[e].rearrange("p (h i) -> p h i", i=128), [], tmp_ws.all().rows, g_const)
            dma("sp", tmp_bs.all().ap, bsT_d[e].partition_broadcast(128).rearrange("p (h i) -> p h i", i=128), [],
                tmp_bs.all().rows, g_const)
            for h in range(NH):
                tt("dve", wct[h], tmp_ws[h], tril.all(), ALU.mult)
            for half in range(2):
                psr = bank[6 + half]
                mm(psr, onesb, Buf(wct.ap_full[:, half * 4:(half + 1) * 4, :].rearrange("p h i -> p (h i)"),
                                  wct.all().rows), True, True)
                for hh in range(4):
                    h = half * 4 + hh
                    stt(cbs[h], psr[hh * 128:(hh + 1) * 128], pv(PV_LVB + e * 8 + h), tmp_bs[h], ALU.mult, ALU.add)

        def load_xs(src, ti, deps):
            dma("sp", xs.all().ap, src[:, :, ti * TT:(ti + 1) * TT].rearrange("c p t -> p c t"), deps, xs.all().rows, g_xs)

        def store_xs(ti, grp=None):
            dma("sp", outT[:, :, ti * TT:(ti + 1) * TT].rearrange("c p t -> p c t"), xs.all().ap, xs.all().rows,
                [t_out[ti]], grp if grp is not None else g_st)

        def phase_a(l, ti, after_hT=None):
            e = l // 2
            t0 = ti * TT
            wn = "w_in%d" % e
            wb = w_in_b[e]
            rms_from_xs()
            make_hT(l, 1)
            if after_hT is not None:
                after_hT()
            if PA_STOP <= 1:
                return
            for cbk in range(4):
                (W,) = get_block([wsrc(wb, cdeps(wn, cbk * 512, 512), 0, 16, cbk * 512, 512)])
                for qi in range(4):
                    hc = cbk * 4 + qi
                    ps = next_ps()
                    for kc in range(DC):
                        mm(ps, W[kc, qi * 128:(qi + 1) * 128], hT[kc], kc == 0, kc == DC - 1)
                    sg = stg[hc % 2]
                    copy("act" if hc % 2 == 0 else "dve", sg, ps)
                    if hc < 8:
                        dma("sp", qT_s[hc][:, t0:t0 + TT], sg.ap, sg.rows, [t_q[hc][ti]], g_scr)
                    else:
                        dma("sp", kT_s[hc - 8][:, t0:t0 + TT], sg.ap, sg.rows, [t_k[hc - 8][ti]], g_scr)
            if PA_STOP <= 2:
                return
            for cbk in range(2):
                c0 = 2048 + cbk * 512
                (W,) = get_block([wsrc(wb, cdeps(wn, c0, 512), 0, 16, c0, 512)])
                for tb in range(4):
                    ps = next_ps()
                    for kc in range(DC):
                        mm(ps, hT[kc, tb * 128:(tb + 1) * 128], W[kc], kc == 0, kc == DC - 1)
                    copy("act" if tb % 2 == 0 else "dve", vstg[tb, cbk * 512:(cbk + 1) * 512], ps)
            for tb in range(4):
                r0 = t0 + tb * 128
                dma("sp", V_s[r0:r0 + 128, :], vstg[tb].ap, vstg[tb].rows, [t_v[r0 // 128]], g_scr)
            if PA_STOP <= 3:
                return
            for cbk in range(2):
                c0 = 3072 + cbk * 512
                (W,) = get_block([wsrc(wb, cdeps(wn, c0, 512), 0, 16, c0, 512)])
                for qi in range(4):
                    ps = next_ps()
                    for kc in range(DC):
                        mm(ps, W[kc, qi * 128:(qi + 1) * 128], hT[kc], kc == 0, kc == DC - 1)
                    act(uT[cbk * 4 + qi], ps, AF.Gelu_apprx_tanh)
            if PA_STOP <= 4:
                return
            (W0,) = get_block([wsrc(wb, cdeps(wn, 4096, 512), 0, 16, 4096, 512)])
            (W1,) = get_block([wsrc(wb, cdeps(wn, 4608, 512), 0, 16, 4608, 512)])
            for tb in range(4):
                sl = tb % 2
                for half, W in enumerate((W0, W1)):
                    ps = next_ps()
                    for kc in range(DC):
                        mm(ps, hT[kc, tb * 128:(tb + 1) * 128], W[kc], kc == 0, kc == DC - 1)
                    act(gbuf[sl, half * 512:(half + 1) * 512], ps, AF.Gelu_apprx_tanh)
                    gsl = gbuf[sl, half * 512:(half + 1) * 512]
                    bo = bnst[sl, half * 6:(half + 1) * 6]
                    op("dve", lambda e, bo=bo, gsl=gsl: e.bn_stats(out=bo.ap, in_=gsl.ap), reads=gsl.rows, writes=bo.rows)
                if PA_STOP <= 5:
                    continue
                mv = bnst[sl, 12:14]
                b12 = bnst[sl, 0:12]
                op("dve", lambda e, mv=mv, b12=b12: e.bn_aggr(out=mv.ap, in_=b12.ap), reads=b12.rows, writes=mv.rows)
                if PA_STOP <= 6:
                    continue
                rs_ = bnst[sl, 14:15]
                nb_ = bnst[sl, 15:16]
                act(rs_, bnst[sl, 13:14], AF.Sqrt, bias=epsb[0:1], scale=1.0)
                op("dve", lambda e, rs_=rs_: e.reciprocal(out=rs_.ap, in_=rs_.ap), reads=rs_.rows, writes=rs_.rows)
                stt(nb_, bnst[sl, 12:13], -1.0, rs_, ALU.mult, ALU.mult)
                if PA_STOP <= 7:
                    continue
                act(gln[sl], gbuf[sl], AF.Identity, bias=nb_, scale=rs_)
                if PA_STOP <= 8:
                    continue
                for half in range(2):
                    psp = bank[6 + half]
                    for hh in range(4):
                        grp = half * 4 + hh
                        mm(psp[hh * 128:(hh + 1) * 128], gln[sl, grp * 128:(grp + 1) * 128], wct[grp], True, True)
                    for hh in range(4):
                        grp = half * 4 + hh
                        tm = sptmp[hh % 2]
                        if PA_STOP == 83:
                            continue
                        stt(tm, psp[hh * 128:(hh + 1) * 128], pv(PV_LVG + e * 8 + grp), cbs[grp], ALU.mult, ALU.add)
                        if PA_STOP == 86:
                            continue
                        tt("dve", bmstg[grp, tb * 128:(tb + 1) * 128], tm, uT[grp, tb * 128:(tb + 1) * 128], ALU.mult)
            if PA_STOP <= 9 or PA_STOP in (83, 86):
                return
            dma("sp", bmT_s[:, :, t0:t0 + TT].rearrange("h p t -> p h t"), bmstg.all().ap, bmstg.all().rows,
                [t_bm[ti]], g_scr)

        def attention(e, preloaded=False, preload_only=False):
            scale = 128.0 ** -0.5
            def att_loads(h):
                hb = h % 2
                dma("sp", qh[hb].all().ap, qT_s[h], t_q[h], qh[hb].all().rows, g_att[hb])
                dma("sp", kh[hb].all().ap, kT_s[h], t_k[h], kh[hb].all().rows, g_att[hb])
                for di, d in enumerate(DILS):
                    nb = 32 // d
                    vsrc = V_s.rearrange("(b i r) c -> r i b c", i=128, r=d)
                    for r in range(d):
                        dstb = vd[hb][di][r * nb:(r + 1) * nb]
                        dma("sp", dstb.ap, vsrc[r][:, :, h * 128:(h + 1) * 128], t_v, dstb.rows, g_att[hb])

            if preload_only:
                att_loads(0)
                return
            if not preloaded:
                att_loads(0)
            for h in range(NH):
                hb = h % 2
                if h + 1 < NH:
                    att_loads(h + 1)
                for di, d in enumerate(DILS):
                    nb = 32 // d
                    if d < 16:
                        groups = [[(r, b) for b in range(g4 * 4, g4 * 4 + 4)] for r in range(d) for g4 in range(nb // 4)]
                    else:
                        groups = [[(r, b) for r in (r0, r0 + 1) for b in range(2)] for r0 in range(0, 16, 2)]
                    qv = lambda r, b: Buf(qh[hb].ap_full[:, r + d * 128 * b: r + d * 128 * b + d * 127 + 1: d], qh[hb].all().rows)
                    kv = lambda r, b: Buf(kh[hb].ap_full[:, r + d * 128 * b: r + d * 128 * b + d * 127 + 1: d], kh[hb].all().rows)
                    blocks = [(gi, bi, r, b) for gi, grp in enumerate(groups) for bi, (r, b) in enumerate(grp)]
                    sidx = [0]

                    def emit_s(r, b):
                        i = sidx[0]
                        sidx[0] += 1
                        pss = bank[i % 4][0:256]
                        p = pT[i % 4]
                        psl = lambda a, c_: Buf(pss.ap[:, a:c_], pss.rows)
                        if b >= 1:
                            mm(psl(0, 128), kv(r, b - 1), qv(r, b), True, False)
                            mm(psl(0, 128), ident, maskP, False, True)
                        mm(psl(128, 256), kv(r, b), qv(r, b), True, False)
                        mm(psl(128, 256), ident, maskC, False, True)
                        if b >= 1:
                            act(p, pss, AF.Exp, scale=scale)
                        else:
                            act(pT[i % 4, 128:256], psl(128, 256), AF.Exp, scale=scale)
                        return i

                    def emit_pv(i, bi, r, b, pso, psd):
                        o_ = pso[bi * 128:(bi + 1) * 128]
                        d_ = psd[bi * 128:(bi + 1) * 128]
                        if b >= 1:
                            mm(o_, vd[hb][di][r * nb + b - 1], pT[i % 4, 0:128], True, False)
                        mm(o_, vd[hb][di][r * nb + b], pT[i % 4, 128:256], b == 0, True)
                        if b >= 1:
                            mm(d_, onesb, pT[i % 4, 0:128], True, False)
                        mm(d_, onesb, pT[i % 4, 128:256], b == 0, True)

                    pendq = []

                    def retire(pend):
                        emit_pv(*pend[:6])
                        if pend[1] == 3:
                            evac_group(e, hb, d, groups[pend[6]], pend[4], pend[5])

                    for n, (gi, bi, r, b) in enumerate(blocks):
                        i = emit_s(r, b)
                        pendq.append((i, bi, r, b, bank[4 + gi % 2], bank[6 + gi % 2], gi))
                        if len(pendq) > 2:
                            retire(pendq.pop(0))
                    while pendq:
                        retire(pendq.pop(0))
                op("dve", lambda e_: e_.reciprocal(out=accd.all().ap, in_=accd.all().ap), reads=accd.all().rows,
                   writes=accd.all().rows)
                tt("dve", aTo.all(), accn.all(), accd.all(), ALU.mult)
                dma("sp", aT_s[h], aTo.all().ap, aTo.all().rows, [t_a[h]], g_ao)

        def evac_group(e, hb, d, grp, pso, psd):
            if d == 1:
                b0 = grp[0][1]
                vn = accn[b0 * 128:(b0 + 4) * 128]
                vdn = accd[b0 * 128:(b0 + 4) * 128]
                copy("act", vn, pso)
                copy("dve", vdn, psd)
                return
            if d == 4:
                r, b0 = grp[0]
                sel = lambda v: v.ap_full.rearrange("p (b i r) -> p r b i", i=128, r=4)[:, r, b0:b0 + 4, :]
                psv = lambda p_: p_.ap_full.rearrange("p (b i) -> p b i", i=128)
            else:
                r0 = grp[0][0]
                sel = lambda v: v.ap_full.rearrange("p (b i r) -> p r b i", i=128, r=16)[:, r0:r0 + 2, :, :]
                psv = lambda p_: p_.ap_full.rearrange("p (r b i) -> p r b i", b=2, i=128)
            an = Buf(sel(accn), accn.all().rows)
            ad = Buf(sel(accd), accd.all().rows)
            tt("dve", an, an, Buf(psv(pso), pso.all().rows), ALU.add)
            tt("dve", ad, ad, Buf(psv(psd), psd.all().rows), ALU.add)

        def out_proj(l, ti):
            e = l // 2
            t0 = ti * TT
            dma("sp", abuf[0:8].ap, aT_s[:, :, t0:t0 + TT].rearrange("h p t -> p h t"), t_a, abuf[0:8].rows, g_ab)
            dma("sp", abuf[8:16].ap, bmT_s[:, :, t0:t0 + TT].rearrange("h p t -> p h t"), [t_bm[ti]], abuf[8:16].rows, g_ab)
            wn = "w_out%d" % e
            for cbk in range(4):
                (W,) = get_block([wsrc(w_out_b[e], cdeps(wn, cbk * 512, 512), 0, 16, cbk * 512, 512)])
                for qi in range(4):
                    c = cbk * 4 + qi
                    ps = next_ps()
                    for kc in range(DC):
                        mm(ps, W[kc, qi * 128:(qi + 1) * 128], abuf[kc], kc == 0, kc == DC - 1)
                    flush_deferred()
                    evac_y(ps, c, None, c == 0, c == DC - 1)
            flush_deferred()
            if OP_STOP == 1:
                return
            finish_rstd(bank[4])
            if OP_STOP == 2:
                return
            x_update(l, 1)

        def ffn(l, ti):
            rms_from_xs()
            make_hT(l, 2)
            wn = "w_gu%d" % l
            for j in range(0, FC, 2):
                G, U = get_block([wsrc(w_gu_b[l], cdeps(wn, j * 128, 256), 0, 16, j * 128, 256),
                                  wsrc(w_gu_b[l], cdeps(wn, DFF + j * 128, 256), 0, 16, DFF + j * 128, 256)])
                for q in range(2):
                    psg = next_ps()
                    for kc in range(DC):
                        mm(psg, G[kc, q * 128:(q + 1) * 128], hT[kc], kc == 0, kc == DC - 1)
                    psu = next_ps()
                    for kc in range(DC):
                        mm(psu, U[kc, q * 128:(q + 1) * 128], hT[kc], kc == 0, kc == DC - 1)
                    s_ = next_sq()
                    act(s_, psg, AF.Silu)
                    tt("dve", actT[j + q], s_, psu, ALU.mult)
            for dcp in range(8):
                (Wa,) = get_block([wsrc(w_dn[l], [], 0, 22, dcp * 256, 256)])
                (Wb,) = get_block([wsrc(w_dn[l], [], 22, 22, dcp * 256, 256)])
                ps0 = next_ps()
                ps1 = next_ps()
                for kc in range(FC):
                    W = Wa if kc < 22 else Wb
                    kk = kc % 22
                    mm(ps0, W[kk, 0:128], actT[kc], kc == 0, kc == FC - 1)
                    mm(ps1, W[kk, 128:256], actT[kc], kc == 0, kc == FC - 1)
                flush_deferred()
                evac_y(ps0, 2 * dcp, None, dcp == 0, False)
                evac_y(ps1, 2 * dcp + 1, None, False, dcp == 7)
            flush_deferred()
            finish_rstd(bank[4])
            x_update(l, 2)

        def mixer_odd(l, ti):
            o = l // 2
            rms_from_xs()
            make_hT(l, 1)
            wn = "w_pw1%d" % o
            copy("dve", Buf(yg.ap_full[:, :, 0:32], yg.all().rows), halo[o])
            for c in range(0, DC, 2):
                Aw, Bw = get_block([wsrc(w_pw1_b[o], cdeps(wn, c * 128, 256), 0, 16, c * 128, 256),
                                    wsrc(w_pw1_b[o], cdeps(wn, D + c * 128, 256), 0, 16, D + c * 128, 256)])
                for q in range(2):
                    psa = next_ps()
                    for kc in range(DC):
                        mm(psa, Aw[kc, q * 128:(q + 1) * 128], hT[kc], kc == 0, kc == DC - 1)
                    psb = next_ps()
                    for kc in range(DC):
                        mm(psb, Bw[kc, q * 128:(q + 1) * 128], hT[kc], kc == 0, kc == DC - 1)
                    s_ = next_sq()
                    act(s_, psb, AF.Sigmoid, bias=pv(PV_BPW1 + o * 32 + 16 + c + q))
                    stt(yg[c + q, 32:544], psa, pv(PV_BPW1 + o * 32 + c + q), s_, ALU.add, ALU.mult)
            copy("dve", halo[o], Buf(yg.ap_full[:, :, 512:544], yg.all().rows))
            for c in range(DC):
                (Dg,) = get_block([(diag_s[o][c].rearrange("p (w n) -> p w n", n=128), [t_diag[o][c]], 31, 128)])
                ps = next_ps()
                for w in range(31):
                    mm(ps, Dg[w], yg[c, 2 + w:2 + w + 512], w == 0, w == 30)
                flush_deferred()
                bdw = pv(PV_BDW + o * 16 + c)
                ts("dve", f1[c], ps, bdw, ALU.add)
                s_ = next_sq()
                act(s_, f1[c], AF.Square)
                deferred.append(lambda s_=s_, c=c: (mm(bank[4], onesf.all(), s_, c == 0, c == DC - 1),
                                                     mm(bank[5], onesf.all(), f1[c], c == 0, c == DC - 1)))
            flush_deferred()
            act(st[1], bank[5], AF.Copy, scale=1.0 / D)
            tt("dve", st[2], st[1], st[1], ALU.mult)
            stt(st[2], bank[4], 1.0 / D, st[2], ALU.mult, ALU.subtract)
            act(st[0], st[2], AF.Sqrt, bias=epsb[0:1], scale=1.0)
            op("dve", lambda e: e.reciprocal(out=st[0].ap, in_=st[0].ap), reads=st[0].rows, writes=st[0].rows)
            for c in range(DC):
                s_ = next_sq()
                tt("dve", s_, f1[c], st[1], ALU.subtract)
                tt("dve", s_, s_, st[0], ALU.mult)
                act(hT2[c], s_, AF.Silu, bias=pv(PV_LCB + o * 16 + c), scale=pv(PV_LCG + o * 16 + c))
            wn = "w_pw2%d" % o
            for cbk in range(4):
                (W,) = get_block([wsrc(w_pw2_b[o], cdeps(wn, cbk * 512, 512), 0, 16, cbk * 512, 512)])
                for qi in range(4):
                    c = cbk * 4 + qi
                    ps = next_ps()
                    for kc in range(DC):
                        mm(ps, W[kc, qi * 128:(qi + 1) * 128], hT2[kc], kc == 0, kc == DC - 1)
                    flush_deferred()
                    evac_y(ps, c, pv(PV_BPW2 + o * 16 + c), c == 0, c == DC - 1)
            flush_deferred()
            finish_rstd(bank[4])
            x_update(l, 1)

        g_dbg = Sd.group("dbg")

        def dump(view, col0):
            n = 1
            for d_ in view.shape:
                n *= d_
            dst = dbg_d[:, col0:col0 + n]
            if len(view.shape) == 2:
                dst = dst.rearrange("p (a b) -> p a b", b=view.shape[1])
            elif len(view.shape) == 3:
                dst = dst.rearrange("p (a b c) -> p a b c", b=view.shape[1], c=view.shape[2])
            dma("pool", dst, view.ap, view.rows, [T()], g_dbg)

        def program():
            ring_state["next_get"] = 0
            ring_state["next_issue"] = 0
            ps_rr[0] = 0
            sq_rr[0] = 0
            if dbg == 10:
                mod_layer(0)
                dump(modp, 0)
                dump(coef, 512)
                dump(cact, 1024)
                return [g_dbg]
            convert("w_in0", w_in[0], w_in_b[0], 5120, 5)
            mod_layer(0)
            build_gmlp_consts(0)
            convert("w_out0", w_out[0], w_out_b[0], D, 2)
            convert("w_gu0", w_gu[0], w_gu_b[0], 2 * DFF, 8)
            convert("w_pw10", w_pw1[0], w_pw1_b[0], 2 * D, 4)
            convert("w_pw20", w_pw2[0], w_pw2_b[0], D, 2)
            convert("w_gu1", w_gu[1], w_gu_b[1], 2 * DFF, 8)
            build_diag(0)
            for ti in range(NT if dbg not in (14, 15) else (1 if dbg == 14 else 2)):
                load_xs(xT, ti, [])
                phase_a(0, ti)
                if dbg == 15:
                    dump(bmstg, ti * 4096)
            if dbg == 15:
                return [g_scr, g_dbg]
            if dbg == 14:
                dump(gbuf, 0)
                dump(gln, 2048)
                dump(bnst, 4096)
                dump(uT, 4608 - 512)
                return [g_scr, g_dbg]
            if dbg == 1:
                return [g_scr]
            attention(0)
            if dbg == 2:
                return [g_ao, g_scr]
            convert("w_in1", w_in[1], w_in_b[1], 5120, 5)
            mod_layer(1)
            mod_layer(2)
            build_gmlp_consts(1)
            nt2 = NT if dbg != 3 else 1
            for ti in range(nt2):
                if ti == 0 or dbg == 3:
                    load_xs(xT, ti, [])
                out_proj(0, ti)
                if dbg == 3:
                    store_xs(ti)
                    return [g_st, g_ao, g_scr]
                ffn(0, ti)
                mixer_odd(1, ti)
                ffn(1, ti)
                store_xs(ti)
                phase_a(2, ti, after_hT=(lambda ti=ti: load_xs(xT, ti + 1, [])) if ti + 1 < NT else None)
            attention(1, preload_only=True)
            convert("w_out1", w_out[1], w_out_b[1], D, 2)
            convert("w_gu2", w_gu[2], w_gu_b[2], 2 * DFF, 8)
            convert("w_pw11", w_pw1[1], w_pw1_b[1], 2 * D, 4)
            convert("w_pw21", w_pw2[1], w_pw2_b[1], D, 2)
            convert("w_gu3", w_gu[3], w_gu_b[3], 2 * DFF, 8)
            build_diag(1)
            attention(1, preloaded=True)
            mod_layer(3)
            for ti in range(NT):
                load_xs(outT, ti, [t_out[ti]])
                out_proj(2, ti)
                ffn(2, ti)
                mixer_odd(3, ti)
                ffn(3, ti)
                store_xs(ti, g_st4)
            return [g_st, g_scr, g_ao]

        Sd.dry = True
        program()
        Sd.dry = False
        deferred.clear()
        fg = program()
        Sd.emit(final_groups=[g for g in Sd.groups if g.count > 0])
    return nc


def _fm(v, nch):
    v = np.asarray(v, np.float32)
    v = v.reshape(v.shape[:-1] + (nch, 128))
    return np.moveaxis(v, -1, 0)


def _host_layout(inp):
    f = lambda a: np.ascontiguousarray(np.asarray(a, np.float32))
    x = f(inp["x"])
    common = {}
    wdw = _fm(inp["w_dw"], DC)
    common["wdwT"] = np.ascontiguousarray(np.transpose(wdw, (1, 0, 3, 2)).reshape(2, 128, DC * 31))
    ident = np.eye(128, dtype=np.float32)
    j = np.arange(128)[:, None]
    i = np.arange(128)[None, :]
    NEG = -30000.0
    maskP = np.where(j >= i, 0.0, NEG).astype(np.float32)
    maskC = np.where(j <= i, 0.0, NEG).astype(np.float32)
    tril = (j <= i).astype(np.float32)
    common["cmat"] = np.ascontiguousarray(np.stack([ident, maskP, maskC, tril], axis=1).reshape(128, 512))
    common["w_mod"] = f(inp["w_mod"])
    common["w_in_ab"] = f(inp["w_in_ab"])
    common["wsT"] = np.ascontiguousarray(np.transpose(f(inp["w_s"]), (0, 3, 1, 2)).reshape(2, 128, NH * 128))
    common["bsT"] = np.ascontiguousarray(np.transpose(f(inp["b_s"]), (0, 2, 1)).reshape(2, NH * 128))
    common["w_out_ab"] = f(inp["w_out_ab"])
    common["w_pw1"] = f(inp["w_pw1"])
    common["w_pw2"] = f(inp["w_pw2"])
    common["w_gate_up"] = f(inp["w_gate_up"])
    common["w_down"] = f(inp["w_down"])
    pv = np.zeros((128, NPV), np.float32)
    pv[:, PV_BMOD:PV_BMOD + 384] = _fm(inp["b_mod"], 96).reshape(128, 384)
    pv[:, PV_NG:PV_NG + 256] = _fm(inp["norm_g"], DC).reshape(128, 256)
    pv[:, PV_BPW1:PV_BPW1 + 64] = _fm(inp["b_pw1"], 32).reshape(128, 64)
    pv[:, PV_BDW:PV_BDW + 32] = _fm(inp["b_dw"], DC).reshape(128, 32)
    pv[:, PV_LCG:PV_LCG + 32] = _fm(inp["ln_c_g"], DC).reshape(128, 32)
    pv[:, PV_LCB:PV_LCB + 32] = _fm(inp["ln_c_b"], DC).reshape(128, 32)
    pv[:, PV_BPW2:PV_BPW2 + 32] = _fm(inp["b_pw2"], DC).reshape(128, 32)
    pv[:, PV_LVG:PV_LVG + 16] = _fm(inp["ln_v_g"], 8).reshape(128, 16)
    pv[:, PV_LVB:PV_LVB + 16] = _fm(inp["ln_v_b"], 8).reshape(128, 16)
    c = f(inp["c"])
    in_maps = []
    for b in range(NCORES):
        m = dict(common)
        m["xT"] = np.ascontiguousarray(x[b].T.reshape(DC, 128, S))
        p = pv.copy()
        p[:, PV_C:PV_C + 16] = _fm(c[b], DC)
        m["pvec"] = p
        in_maps.append(m)
    return in_maps


_NC_CACHE = {}


def kernel(**inputs):
    in_maps = _host_layout(inputs)
    if "nc" not in _NC_CACHE:
        _NC_CACHE["nc"] = build_nc()
    nc = _NC_CACHE["nc"]
    res = run_bass_kernel_spmd(nc, in_maps, core_ids=list(range(NCORES)))
    out = np.empty((NCORES, S, D), np.float32)
    for b in range(NCORES):
        out[b] = res.results[b]["outT"].reshape(D, S).T
    return out
```
